# Optimizing a Trainium2 kernel written in Bass

```python
import math
import jax, jax.numpy as jnp
from jax import lax
import numpy as np

D_MODEL = 1024
BATCH = 4
SEQ = 4096
DEPTH = 2

GRID_W = 64
CTX_LEN = 256
D_FF = 2816
N_MOD = 9
EPS = 1e-6

HY_C = 256
HY_GROUPS = 4
HY_ORDER = 2
HY_BANDS = 16
HY_EMB = 1 + 2 * HY_BANDS
HY_FFN = 64
HY_DECAY_TARGET = 1e-2
HY_FAST_PCT = 0.3
HY_SLOW_PCT = 1.5
HY_WINDOW_SHIFT = 0.05

SG_C = 256
SG_GROUPS = 4
SG_CHUNK = 128

MLA_HEADS = 8
MLA_NOPE = 64
MLA_ROPE = 32
MLA_V = 64
MLA_Q_RANK = 384
MLA_KV_RANK = 256
MLA_SCALE = (MLA_NOPE + MLA_ROPE) ** -0.5
ROPE_BASE = 10000.0
Q_BLOCK = 128

MIX_WIDTH = HY_C + SG_C + MLA_HEADS * MLA_V
OFF_SG = 3 * HY_C
OFF_Q = OFF_SG + 2 * SG_C
OFF_KV = OFF_Q + MLA_Q_RANK
OFF_KR = OFF_KV + MLA_KV_RANK
N_IN = OFF_KR + MLA_ROPE

kernel_name = "hybrid_hyena_sgu_mla_dit_trunk"


def rms(x):
    xf = x.astype(jnp.float32)
    return (xf * lax.rsqrt(jnp.mean(xf * xf, axis=-1, keepdims=True) + EPS)).astype(x.dtype)


def layer_norm(x, g, b):
    xf = x.astype(jnp.float32)
    mu = jnp.mean(xf, axis=-1, keepdims=True)
    var = jnp.mean(jnp.square(xf - mu), axis=-1, keepdims=True)
    return ((xf - mu) * lax.rsqrt(var + EPS)).astype(x.dtype) * g + b


def modulate(x, shift, scale):
    return rms(x) * (1 + scale) + shift


def swiglu(h, wg, wu, wd):
    return (jax.nn.silu(h @ wg) * (h @ wu)) @ wd


def short_conv(u, w, b):
    L = u.shape[1]
    up = jnp.pad(u, ((0, 0), (1, 1), (0, 0)))
    return up[:, 0:L] * w[0] + up[:, 1:L + 1] * w[1] + up[:, 2:L + 2] * w[2] + b


def hyena_filters(L, w1, b1, w2, b2, w3, freq):
    pos = jnp.arange(L, dtype=jnp.float32)
    t = pos / max(L - 1, 1)
    w = 2.0 * math.pi * pos / L
    bands = jnp.linspace(1e-4, HY_BANDS - 1, HY_BANDS, dtype=jnp.float32)
    ang = w[:, None] * bands[None, :]
    z = jnp.concatenate([t[:, None], jnp.cos(ang), -jnp.sin(ang)], axis=-1).astype(w1.dtype)
    h = jnp.sin(freq[0] * (z @ w1 + b1))
    h = jnp.sin(freq[1] * (h @ w2 + b2))
    h = (h @ w3).reshape(L, HY_ORDER, 2, HY_C)
    min_decay = abs(math.log(HY_DECAY_TARGET) / HY_SLOW_PCT)
    max_decay = abs(math.log(HY_DECAY_TARGET) / HY_FAST_PCT)
    deltas = jnp.linspace(min_decay, max_decay, HY_C, dtype=jnp.float32)
    window = jnp.exp(-t[:, None, None, None] * deltas) + HY_WINDOW_SHIFT
    return h * window.astype(h.dtype)


def bidir_fftconv(u, h_fwd, h_bwd):
    L = u.shape[1]
    k = jnp.concatenate([h_fwd[:1] + h_bwd[:1], h_fwd[1:], jnp.zeros_like(h_fwd[:1]), h_bwd[:0:-1]], axis=0)
    uf = jnp.fft.rfft(u.astype(jnp.float32), n=2 * L, axis=1)
    kf = jnp.fft.rfft(k.astype(jnp.float32), n=2 * L, axis=0)
    y = jnp.fft.irfft(uf * kf[None], n=2 * L, axis=1)[:, :L]
    return y.astype(u.dtype)


def hyena_mixer(p3, conv_w, conv_b, filt, bias):
    z = short_conv(p3, conv_w, conv_b)
    x1, x2, v = jnp.split(z, 3, axis=-1)
    gates = (x1, x2)
    for o in range(HY_ORDER):
        v = gates[o] * (bidir_fftconv(v, filt[:, o, 0], filt[:, o, 1]) + v * bias[o])
    return v


def chunk_sgu(p2, ln_g, ln_b, w_s, b_s):
    z = jax.nn.gelu(p2, approximate=False)
    u, v = jnp.split(z, 2, axis=-1)
    v = layer_norm(v, ln_g, ln_b)
    B, L, _ = v.shape
    v = v.reshape(B, L // SG_CHUNK, SG_CHUNK, SG_GROUPS, SG_C // SG_GROUPS)
    s = jnp.einsum('gpq,bnqgc->bnpgc', w_s, v) + b_s.T[None, None, :, :, None]
    return u * s.reshape(B, L, SG_C)


def axial_rope_tables(n):
    n_rows = n // GRID_W
    rows = jnp.repeat(jnp.arange(n_rows, dtype=jnp.float32), GRID_W)
    cols = jnp.tile(jnp.arange(GRID_W, dtype=jnp.float32), n_rows)
    quarter = MLA_ROPE // 4
    inv = 1.0 / (ROPE_BASE ** (jnp.arange(quarter, dtype=jnp.float32) / quarter))
    ang = jnp.stack([rows[:, None] * inv, cols[:, None] * inv], axis=1)
    return jnp.cos(ang), jnp.sin(ang)


def apply_axial_rope(x, cos, sin):
    shp = x.shape
    quarter = MLA_ROPE // 4
    xr = x.reshape(shp[:-1] + (2, 2, quarter))
    x1, x2 = xr[..., 0, :], xr[..., 1, :]
    bshape = (shp[1],) + (1,) * (x.ndim - 3) + (2, quarter)
    c = cos.reshape(bshape).astype(x.dtype)
    s = sin.reshape(bshape).astype(x.dtype)
    return jnp.stack([x1 * c - x2 * s, x2 * c + x1 * s], axis=-2).reshape(shp)


def mla_q(cq, g, w_uq):
    B, n, _ = cq.shape
    q = ((rms(cq) * g) @ w_uq).reshape(B, n, MLA_HEADS, MLA_NOPE + MLA_ROPE)
    return q[..., :MLA_NOPE], q[..., MLA_NOPE:]


def mla_kv(ckv, g, w_ukv):
    B, n, _ = ckv.shape
    kv = ((rms(ckv) * g) @ w_ukv).reshape(B, n, MLA_HEADS, MLA_NOPE + MLA_V)
    return kv[..., :MLA_NOPE], kv[..., MLA_NOPE:]


def attend(q_nope, q_rope, k_nope, k_rope, v):
    s = jnp.einsum('bqhd,bkhd->bhqk', q_nope, k_nope) + jnp.einsum('bqhr,bkr->bhqk', q_rope, k_rope)
    p = jax.nn.softmax(s.astype(jnp.float32) * MLA_SCALE, axis=-1).astype(v.dtype)
    return jnp.einsum('bhqk,bkhd->bqhd', p, v)


def attend_blocks(q_nope, q_rope, k_nope, k_rope, v):
    B, n = q_nope.shape[:2]
    nb = n // Q_BLOCK

    def to_blocks(t):
        return t.reshape((B, nb, Q_BLOCK) + t.shape[2:]).swapaxes(0, 1)

    out = lax.map(lambda qs: attend(qs[0], qs[1], k_nope, k_rope, v), (to_blocks(q_nope), to_blocks(q_rope)))
    return out.swapaxes(0, 1).reshape(B, n, MLA_HEADS * MLA_V)


def setup_inputs(seed: int = 0) -> dict:
    key = jax.random.key(seed)
    ks = iter(jax.random.split(key, 40))

    def nrm(shape, scale):
        return scale * jax.random.normal(next(ks), shape, jnp.float32)

    D = D_MODEL
    return {
        "x": nrm((BATCH, SEQ, D), 1.0),
        "c": nrm((BATCH, D), 1.0),
        "ctx": nrm((BATCH, CTX_LEN, D), 1.0),
        "c_ctx": nrm((D,), 1.0),
        "ada_w": nrm((DEPTH, D, N_MOD * D), 0.5 * D ** -0.5),
        "ada_b": nrm((DEPTH, N_MOD * D), 0.02),
        "ffn1_wg": nrm((DEPTH, D, D_FF), D ** -0.5),
        "ffn1_wu": nrm((DEPTH, D, D_FF), D ** -0.5),
        "ffn1_wd": nrm((DEPTH, D_FF, D), D_FF ** -0.5),
        "w_in": nrm((DEPTH, D, N_IN), D ** -0.5),
        "w_out": nrm((DEPTH, MIX_WIDTH, D), MIX_WIDTH ** -0.5),
        "hy_conv_w": nrm((DEPTH, 3, 3 * HY_C), 3 ** -0.5),
        "hy_conv_b": nrm((DEPTH, 3 * HY_C), 0.02),
        "hy_w1": nrm((DEPTH, HY_EMB, HY_FFN), 2.0 * HY_EMB ** -0.5),
        "hy_b1": nrm((DEPTH, HY_FFN), 0.1),
        "hy_w2": nrm((DEPTH, HY_FFN, HY_FFN), 2.0 * HY_FFN ** -0.5),
        "hy_b2": nrm((DEPTH, HY_FFN), 0.1),
        "hy_w3": nrm((DEPTH, HY_FFN, HY_ORDER * 2 * HY_C), 0.005),
        "hy_freq": 1.0 + nrm((DEPTH, 2, HY_FFN), 0.1),
        "hy_bias": nrm((DEPTH, HY_ORDER, HY_C), 0.2),
        "sg_ln_g": 1.0 + nrm((DEPTH, SG_C), 0.05),
        "sg_ln_b": nrm((DEPTH, SG_C), 0.02),
        "sg_w": nrm((DEPTH, SG_GROUPS, SG_CHUNK, SG_CHUNK), 0.5 * SG_CHUNK ** -0.5),
        "sg_b": 1.0 + nrm((DEPTH, SG_GROUPS, SG_CHUNK), 0.1),
        "mla_q_norm": 1.0 + nrm((DEPTH, MLA_Q_RANK), 0.05),
        "mla_w_uq": nrm((DEPTH, MLA_Q_RANK, MLA_HEADS * (MLA_NOPE + MLA_ROPE)), MLA_Q_RANK ** -0.5),
        "mla_kv_norm": 1.0 + nrm((DEPTH, MLA_KV_RANK), 0.05),
        "mla_w_ukv": nrm((DEPTH, MLA_KV_RANK, MLA_HEADS * (MLA_NOPE + MLA_V)), MLA_KV_RANK ** -0.5),
        "ffn2_wg": nrm((DEPTH, D, D_FF), D ** -0.5),
        "ffn2_wu": nrm((DEPTH, D, D_FF), D ** -0.5),
        "ffn2_wd": nrm((DEPTH, D_FF, D), D_FF ** -0.5),
        "final_norm": 1.0 + nrm((D,), 0.05),
    }


def reference(x, c, ctx, c_ctx, ada_w, ada_b, ffn1_wg, ffn1_wu, ffn1_wd, w_in, w_out,
              hy_conv_w, hy_conv_b, hy_w1, hy_b1, hy_w2, hy_b2, hy_w3, hy_freq, hy_bias,
              sg_ln_g, sg_ln_b, sg_w, sg_b, mla_q_norm, mla_w_uq, mla_kv_norm, mla_w_ukv,
              ffn2_wg, ffn2_wu, ffn2_wd, final_norm):
    B, n_lat, D = x.shape
    n_ctx = ctx.shape[1]
    cos, sin = axial_rope_tables(n_lat)
    xc = ctx
    for l in range(DEPTH):
        last = l == DEPTH - 1
        mod = (jax.nn.silu(c) @ ada_w[l] + ada_b[l]).reshape(B, N_MOD, 1, D)
        modc = (jax.nn.silu(c_ctx) @ ada_w[l] + ada_b[l]).reshape(N_MOD, D)

        x = x + 0.5 * mod[:, 2] * swiglu(modulate(x, mod[:, 0], mod[:, 1]), ffn1_wg[l], ffn1_wu[l], ffn1_wd[l])
        xc = xc + 0.5 * modc[2] * swiglu(modulate(xc, modc[0], modc[1]), ffn1_wg[l], ffn1_wu[l], ffn1_wd[l])

        h = modulate(x, mod[:, 3], mod[:, 4])
        hc = modulate(xc, modc[3], modc[4])
        proj = h @ w_in[l]
        projc = hc @ (w_in[l][:, OFF_KV:] if last else w_in[l])

        kc_nope, vc = mla_kv(projc[..., -(MLA_KV_RANK + MLA_ROPE):-MLA_ROPE], mla_kv_norm[l], mla_w_ukv[l])
        kc_rope = projc[..., -MLA_ROPE:]

        k_nope, v = mla_kv(proj[..., OFF_KV:OFF_KR], mla_kv_norm[l], mla_w_ukv[l])
        k_rope = apply_axial_rope(proj[..., OFF_KR:N_IN], cos, sin)
        q_nope, q_rope = mla_q(proj[..., OFF_Q:OFF_KV], mla_q_norm[l], mla_w_uq[l])
        q_rope = apply_axial_rope(q_rope, cos, sin)
        y_attn = attend_blocks(q_nope, q_rope,
                               jnp.concatenate([kc_nope, k_nope], axis=1),
                               jnp.concatenate([kc_rope, k_rope], axis=1),
                               jnp.concatenate([vc, v], axis=1))
        filt = hyena_filters(n_lat, hy_w1[l], hy_b1[l], hy_w2[l], hy_b2[l], hy_w3[l], hy_freq[l])
        y_hy = hyena_mixer(proj[..., :OFF_SG], hy_conv_w[l], hy_conv_b[l], filt, hy_bias[l])
        y_sg = chunk_sgu(proj[..., OFF_SG:OFF_Q], sg_ln_g[l], sg_ln_b[l], sg_w[l], sg_b[l])
        x = x + mod[:, 5] * (jnp.concatenate([y_hy, y_sg, y_attn], axis=-1) @ w_out[l])

        if not last:
            filt_c = hyena_filters(n_ctx, hy_w1[l], hy_b1[l], hy_w2[l], hy_b2[l], hy_w3[l], hy_freq[l])
            yc_hy = hyena_mixer(projc[..., :OFF_SG], hy_conv_w[l], hy_conv_b[l], filt_c, hy_bias[l])
            yc_sg = chunk_sgu(projc[..., OFF_SG:OFF_Q], sg_ln_g[l], sg_ln_b[l], sg_w[l], sg_b[l])
            qc_nope, qc_rope = mla_q(projc[..., OFF_Q:OFF_KV], mla_q_norm[l], mla_w_uq[l])
            yc_attn = attend(qc_nope, qc_rope, kc_nope, kc_rope, vc).reshape(B, n_ctx, MLA_HEADS * MLA_V)
            xc = xc + modc[5] * (jnp.concatenate([yc_hy, yc_sg, yc_attn], axis=-1) @ w_out[l])

        x = x + 0.5 * mod[:, 8] * swiglu(modulate(x, mod[:, 6], mod[:, 7]), ffn2_wg[l], ffn2_wu[l], ffn2_wd[l])
        if not last:
            xc = xc + 0.5 * modc[8] * swiglu(modulate(xc, modc[6], modc[7]), ffn2_wg[l], ffn2_wu[l], ffn2_wd[l])
    return rms(x) * final_norm
```

```python
import numpy as np
from contextlib import ExitStack
import concourse.bass as bass
import concourse.mybir as mybir
from concourse.bass_utils import run_bass_kernel_spmd

F32 = mybir.dt.float32
BF16 = mybir.dt.bfloat16
AF = mybir.ActivationFunctionType
ALU = mybir.AluOpType
AX = mybir.AxisListType

NDS = 24


class K:
    def __init__(self, nc):
        self.nc = nc
        self.es = ExitStack()
        self.eng = {"pe": nc.tensor, "act": nc.scalar, "dve": nc.vector,
                    "pool": nc.gpsimd, "sp": nc.sync}
        self.sem = {e: nc.alloc_semaphore("s_" + e) for e in self.eng}
        self.cnt = {e: 0 for e in self.eng}
        self.dsem = [nc.alloc_semaphore("d%d" % i) for i in range(NDS)]
        self.dval = [0] * NDS
        self.dn = 0
        self.waited = {e: {} for e in self.eng}
        self.lastw = {}
        self.reads = {}
        self.uid = 0

    def sb(self, shape, dt, name=None):
        self.uid += 1
        return self.es.enter_context(self.nc.sbuf_tensor(name or "sb%d" % self.uid, list(shape), dt))

    def ps(self, shape, dt=F32, name=None):
        self.uid += 1
        return self.es.enter_context(self.nc.psum_tensor(name or "ps%d" % self.uid, list(shape), dt))

    def _wait(self, e, tok):
        sem, val = tok
        w = self.waited[e]
        if w.get(sem.num, 0) >= val:
            return
        self.eng[e].wait_ge(sem, val)
        w[sem.num] = val

    def _deps(self, e, reads, writes, pe_accum=False):
        toks = []
        for k in reads:
            t = self.lastw.get(k)
            if t is not None:
                toks.append(t)
        for k in writes:
            t = self.lastw.get(k)
            if t is not None:
                toks.append(t)
            for t in self.reads.get(k, ()):
                toks.append(t)
        for t in toks:
            if pe_accum and t[0] is self.sem["pe"]:
                continue
            self._wait(e, t)

    def _commit(self, tok, reads, writes):
        for k in writes:
            self.lastw[k] = tok
            self.reads[k] = []
        for k in reads:
            if k in writes:
                continue
            self.reads.setdefault(k, []).append(tok)
            if len(self.reads[k]) > 12:
                best = {}
                for s, v in self.reads[k]:
                    if s.num not in best or best[s.num][1] < v:
                        best[s.num] = (s, v)
                self.reads[k] = list(best.values())

    def op(self, e, fn, reads=(), writes=(), pe_accum=False):
        self._deps(e, reads, writes, pe_accum)
        ins = fn(self.eng[e])
        self.cnt[e] += 1
        ins.then_inc(self.sem[e], 1)
        tok = (self.sem[e], self.cnt[e])
        self._commit(tok, reads, writes)
        return tok

    def dma(self, q, out, in_, reads=(), writes=(), **kw):
        i = self.dn % NDS
        self.dn += 1
        if self.dval[i] > 0:
            self._wait(q, (self.dsem[i], self.dval[i]))
        self._deps(q, reads, writes)
        ins = self.eng[q].dma_start(out=out, in_=in_, **kw)
        self.dval[i] += 16
        ins.then_inc(self.dsem[i], 16)
        tok = (self.dsem[i], self.dval[i])
        self._commit(tok, reads, writes)
        return tok

    def barrier(self):
        toks = [(self.sem[e], self.cnt[e]) for e in self.eng if self.cnt[e] > 0]
        toks += [(self.dsem[i], self.dval[i]) for i in range(NDS) if self.dval[i] > 0]
        for e in self.eng:
            for t in toks:
                self._wait(e, t)
        self.lastw = {}
        self.reads = {}

    def finish(self, out_toks=None):
        self.barrier()


D = 1024
DFF = 2816
NFF = DFF // 128
T = 2304
TILES = [(0, 512, 0), (512, 512, 0), (1024, 512, 0), (1536, 512, 0), (2048, 256, 1)]
NPROJ = 1984
EPS = 1e-6


class Ctx:
    pass


def setup_common(k):
    g = Ctx()
    g.bank = [k.ps([128, 512], F32, name="bank%d" % i) for i in range(8)]
    g.ones = k.sb([128, 128], BF16, name="ones")
    k.op("dve", lambda e: e.memset(g.ones[:], 1.0), writes=["ones"])
    g.epsb = k.sb([128, 1], F32, name="epsb")
    k.op("dve", lambda e: e.memset(g.epsb[:], EPS), writes=["epsb"])
    g.mod = k.sb([128, 72, 2], F32, name="mod")
    g.modp1 = k.sb([128, 72, 2], F32, name="modp1")
    g.modh = k.sb([128, 72, 2], F32, name="modh")
    return g


def rms_rstd(k, g, src, nk, w, bank_i, rstd, tag, sq, extra=()):
    k.op("act", lambda e: e.activation(out=sq[:, 0:nk, 0:w], in_=src[:, 0:nk, 0:w], func=AF.Square),
         reads=[tag + "src"], writes=[tag + "sq"] + list(extra))
    pb = g.bank[bank_i]
    for kc in range(nk):
        k.op("pe", lambda e: e.matmul(pb[:, 0:w], lhsT=g.ones[:], rhs=sq[:, kc, 0:w], start=(kc == 0), stop=(kc == nk - 1)),
             reads=["ones", tag + "sq"], writes=["bank%d" % bank_i], pe_accum=True)
    k.op("act", lambda e: e.activation(out=rstd[:, 0:w], in_=pb[:, 0:w], func=AF.Sqrt, bias=g.epsb[:, 0:1], scale=1.0 / (nk * 128)),
         reads=["bank%d" % bank_i, "epsb"], writes=[tag + "rstd"])
    k.op("dve", lambda e: e.reciprocal(out=rstd[:, 0:w], in_=rstd[:, 0:w]),
         reads=[tag + "rstd"], writes=[tag + "rstd"])


def load_mod(k, g, modT_dram):
    k.dma("sp", g.mod[:].rearrange("p a b -> p (a b)"), modT_dram, writes=["mod"])
    derive_mod(k, g)


def derive_mod(k, g):
    k.op("dve", lambda e: e.tensor_scalar(out=g.modp1[:], in0=g.mod[:], scalar1=1.0, scalar2=None, op0=ALU.add),
         reads=["mod"], writes=["modp1"])
    k.op("dve", lambda e: e.tensor_scalar(out=g.modh[:], in0=g.mod[:], scalar1=0.5, scalar2=None, op0=ALU.mult),
         reads=["mod"], writes=["modh"])


def mod_phase(k, g, cT, ada_w, ada_bT, modT_out):
    with ExitStack() as es:
        old = k.es
        k.es = es
        c_sb = k.sb([128, 8, 2], F32, name="c_sb")
        cs = k.sb([128, 8, 2], BF16, name="cs")
        ab = k.sb([128, 72], F32, name="ab")
        aw = [k.sb([128, 8, 1024], BF16, name="aw%d" % i) for i in range(2)]
        k.dma("sp", c_sb[:], cT.rearrange("(kc p) c -> p kc c", p=128), writes=["c_sb"])
        k.dma("sp", ab[:], ada_bT, writes=["ab"])
        k.op("act", lambda e: e.activation(out=cs[:], in_=c_sb[:], func=AF.Silu), reads=["c_sb"], writes=["cs"])
        pm = g.bank[7]
        awv = ada_w.rearrange("(kc p) n -> p kc n", p=128)
        for n in range(9):
            a = aw[n % 2]
            ak = "aw%d" % (n % 2)
            for kc in range(8):
                k.dma("pool", a[:, kc, :], awv[:, kc, n * 1024:(n + 1) * 1024], writes=[ak])
            for fc in range(8):
                j = n * 8 + fc
                for kc in range(8):
                    k.op("pe", lambda e: e.matmul(pm[:, 2 * j:2 * j + 2], lhsT=a[:, kc, fc * 128:(fc + 1) * 128], rhs=cs[:, kc, :],
                                                  start=(kc == 0), stop=(kc == 7)),
                         reads=[ak, "cs"], writes=["bank7"], pe_accum=True)
        pmv = pm[:, 0:144].rearrange("p (j c) -> p j c", c=2)
        for col in range(2):
            k.op("dve", lambda e: e.tensor_tensor(out=g.mod[:, :, col], in0=pmv[:, :, col], in1=ab[:], op=ALU.add),
                 reads=["bank7", "ab"], writes=["mod"])
        derive_mod(k, g)
        k.dma("sp", modT_out, g.mod[:].rearrange("p a b -> p (a b)"), reads=["mod"])
        k.barrier()
        k.es = old


def modulate_tile(k, g, x_sb, h_sb, rstd, tmp, nk, w, n_shift, n_scale, mc, tag, chunk0=0):
    for kc in range(nk):
        tb = tmp[kc % 2]
        tk = tag + "tmp%d" % (kc % 2)
        js = n_scale * 8 + chunk0 + kc
        jb = n_shift * 8 + chunk0 + kc
        k.op("dve", lambda e: e.scalar_tensor_tensor(out=tb[:, 0:w], in0=x_sb[:, kc, 0:w], scalar=g.modp1[:, js, mc:mc + 1],
                                                    in1=rstd[:, 0:w], op0=ALU.mult, op1=ALU.mult),
             reads=[tag + "src", "modp1", tag + "rstd"], writes=[tk])
        k.op("act", lambda e: e.activation(out=h_sb[:, kc, 0:w], in_=tb[:, 0:w], func=AF.Identity,
                                           bias=g.mod[:, jb, mc:mc + 1], scale=1.0),
             reads=[tk, "mod"], writes=[tag + "h"])


def ffn_phase(k, g, xin, xout, wg, wu, wd, n_shift, n_scale, n_gate, final_norm=None, tag="f"):
    with ExitStack() as es:
        old = k.es
        k.es = es
        wg_sb = k.sb([128, 8, DFF], BF16, name=tag + "wg")
        wu_sb = k.sb([128, 8, DFF], BF16, name=tag + "wu")
        wd_sb = k.sb([128, NFF, D], BF16, name=tag + "wd")
        x_sb = k.sb([128, 8, 512], F32, name=tag + "x")
        h_sb = k.sb([128, 8, 512], BF16, name=tag + "h")
        a_sb = k.sb([128, NFF, 512], BF16, name=tag + "a")
        rstd = k.sb([128, 512], F32, name=tag + "rstd")
        tmp = [k.sb([128, 512], F32, name=tag + "tmp%d" % i) for i in range(2)]
        sgs = [k.sb([128, 512], F32, name=tag + "sg%d" % i) for i in range(2)]
        fn_sb = None
        if final_norm is not None:
            fn_sb = k.sb([128, 8], F32, name=tag + "fn")
            k.dma("sp", fn_sb[:], final_norm, writes=["fn"])
        wgv = wg.rearrange("(kc p) n -> p kc n", p=128)
        wuv = wu.rearrange("(kc p) n -> p kc n", p=128)
        wdv = wd.rearrange("(nc p) o -> p nc o", p=128)
        for kc in range(8):
            k.dma("pool", wg_sb[:, kc, :], wgv[:, kc, :], writes=[tag + "wg"])
        for kc in range(8):
            k.dma("pool", wu_sb[:, kc, :], wuv[:, kc, :], writes=[tag + "wu"])
        for n in range(NFF):
            k.dma("pool", wd_sb[:, n, :], wdv[:, n, :], writes=[tag + "wd"])
        xinv = xin.rearrange("(kc p) t -> p kc t", p=128)
        xoutv = xout.rearrange("(kc p) t -> p kc t", p=128)
        for (c0, w, mc) in TILES:
            k.dma("sp", x_sb[:, :, 0:w], xinv[:, :, c0:c0 + w], writes=[tag + "src"])
            rms_rstd(k, g, x_sb, 8, w, 0, rstd, tag, a_sb, extra=[tag + "a%d" % n for n in range(8)])
            modulate_tile(k, g, x_sb, h_sb, rstd, tmp, 8, w, n_shift, n_scale, mc, tag)
            for n in range(NFF):
                bg = 1 + (n % 2)
                bu = 3 + (n % 2)
                pg = g.bank[bg]
                pu = g.bank[bu]
                for kc in range(8):
                    k.op("pe", lambda e: e.matmul(pg[:, 0:w], lhsT=wg_sb[:, kc, n * 128:(n + 1) * 128], rhs=h_sb[:, kc, 0:w],
                                                  start=(kc == 0), stop=(kc == 7)),
                         reads=[tag + "wg", tag + "h"], writes=["bank%d" % bg], pe_accum=True)
                for kc in range(8):
                    k.op("pe", lambda e: e.matmul(pu[:, 0:w], lhsT=wu_sb[:, kc, n * 128:(n + 1) * 128], rhs=h_sb[:, kc, 0:w],
                                                  start=(kc == 0), stop=(kc == 7)),
                         reads=[tag + "wu", tag + "h"], writes=["bank%d" % bu], pe_accum=True)
                sg = sgs[n % 2]
                sk = tag + "sg%d" % (n % 2)
                k.op("act", lambda e: e.activation(out=sg[:, 0:w], in_=pg[:, 0:w], func=AF.Silu),
                     reads=["bank%d" % bg], writes=[sk])
                k.op("dve", lambda e: e.tensor_tensor(out=a_sb[:, n, 0:w], in0=sg[:, 0:w], in1=pu[:, 0:w], op=ALU.mult),
                     reads=[sk, "bank%d" % bu, tag + "sq"], writes=[tag + "a%d" % n, tag + "sq"] if n < 8 else [tag + "a%d" % n])
            for oc in range(8):
                bi = 5 + (oc % 2)
                py = g.bank[bi]
                for n in range(NFF):
                    k.op("pe", lambda e: e.matmul(py[:, 0:w], lhsT=wd_sb[:, n, oc * 128:(oc + 1) * 128], rhs=a_sb[:, n, 0:w],
                                                  start=(n == 0), stop=(n == NFF - 1)),
                         reads=[tag + "wd", tag + "a%d" % n], writes=["bank%d" % bi], pe_accum=True)
                j = n_gate * 8 + oc
                k.op("dve", lambda e: e.scalar_tensor_tensor(out=x_sb[:, oc, 0:w], in0=py[:, 0:w], scalar=g.modh[:, j, mc:mc + 1],
                                                            in1=x_sb[:, oc, 0:w], op0=ALU.mult, op1=ALU.add),
                     reads=["bank%d" % bi, "modh", tag + "src"], writes=[tag + "src"])
            if final_norm is not None:
                rms_rstd(k, g, x_sb, 8, w, 0, rstd, tag, a_sb, extra=[tag + "a%d" % n for n in range(8)])
                for kc in range(8):
                    k.op("dve", lambda e: e.scalar_tensor_tensor(out=x_sb[:, kc, 0:w], in0=x_sb[:, kc, 0:w], scalar=fn_sb[:, kc:kc + 1],
                                                                in1=rstd[:, 0:w], op0=ALU.mult, op1=ALU.mult),
                         reads=[tag + "src", "fn", tag + "rstd"], writes=[tag + "src"])
            k.dma("sp", xoutv[:, :, c0:c0 + w], x_sb[:, :, 0:w], reads=[tag + "src"])
        k.barrier()
        k.es = old


def proj_phase(k, g, xin, w_inx, projT, tag="p"):
    with ExitStack() as es:
        old = k.es
        k.es = es
        w_sb = k.sb([128, 8, NPROJ], BF16, name=tag + "w")
        x_sb = k.sb([128, 8, 512], F32, name=tag + "x")
        sq = k.sb([128, 8, 512], BF16, name=tag + "sq")
        h_sb = k.sb([128, 8, 512], BF16, name=tag + "h")
        o_sb = k.sb([128, 16, 512], F32, name=tag + "o")
        rstd = k.sb([128, 512], F32, name=tag + "rstd")
        tmp = [k.sb([128, 512], F32, name=tag + "tmp%d" % i) for i in range(2)]
        wv = w_inx.rearrange("(kc p) n -> p kc n", p=128)
        for kc in range(8):
            k.dma("pool", w_sb[:, kc, :], wv[:, kc, :], writes=[tag + "w"])
        k.op("dve", lambda e: e.memset(o_sb[:, 15, :], 0.0), writes=[tag + "o"])
        xinv = xin.rearrange("(kc p) t -> p kc t", p=128)
        pv = projT.rearrange("(c p) t -> p c t", p=128)
        for (c0, w, mc) in TILES:
            k.dma("sp", x_sb[:, :, 0:w], xinv[:, :, c0:c0 + w], writes=[tag + "src"])
            rms_rstd(k, g, x_sb, 8, w, 0, rstd, tag, sq)
            modulate_tile(k, g, x_sb, h_sb, rstd, tmp, 8, w, 3, 4, mc, tag)
            for oc in range(16):
                m = 128 if oc < 15 else 64
                bi = 1 + (oc % 4)
                pb = g.bank[bi]
                for kc in range(8):
                    k.op("pe", lambda e: e.matmul(pb[0:m, 0:w], lhsT=w_sb[:, kc, oc * 128:oc * 128 + m], rhs=h_sb[:, kc, 0:w],
                                                  start=(kc == 0), stop=(kc == 7)),
                         reads=[tag + "w", tag + "h"], writes=["bank%d" % bi], pe_accum=True)
                eng = "act" if oc % 2 == 0 else "dve"
                if eng == "act":
                    k.op("act", lambda e: e.copy(out=o_sb[0:m, oc, 0:w], in_=pb[0:m, 0:w]), reads=["bank%d" % bi], writes=[tag + "o"])
                else:
                    k.op("dve", lambda e: e.tensor_copy(out=o_sb[0:m, oc, 0:w], in_=pb[0:m, 0:w]), reads=["bank%d" % bi], writes=[tag + "o"])
            k.dma("sp", pv[:, :, c0:c0 + w], o_sb[:, :, 0:w], reads=[tag + "o"])
        k.barrier()
        k.es = old


def outproj_phase(k, g, xin, yT, w_out, xout, tag="o"):
    with ExitStack() as es:
        old = k.es
        k.es = es
        w_sb = k.sb([128, 8, D], BF16, name=tag + "w")
        x_sb = k.sb([128, 8, 512], F32, name=tag + "x")
        y_sb = k.sb([128, 8, 512], BF16, name=tag + "y")
        wv = w_out.rearrange("(kc p) n -> p kc n", p=128)
        k.dma("pool", w_sb[:], wv, writes=[tag + "w"])
        xinv = xin.rearrange("(kc p) t -> p kc t", p=128)
        yv = yT.rearrange("(kc p) t -> p kc t", p=128)
        xoutv = xout.rearrange("(kc p) t -> p kc t", p=128)
        for (c0, w, mc) in TILES:
            k.dma("sp", x_sb[:, :, 0:w], xinv[:, :, c0:c0 + w], writes=[tag + "x"])
            k.dma("pool", y_sb[:, :, 0:w], yv[:, :, c0:c0 + w], writes=[tag + "y"])
            for oc in range(8):
                bi = 1 + (oc % 4)
                pb = g.bank[bi]
                for kc in range(8):
                    k.op("pe", lambda e: e.matmul(pb[:, 0:w], lhsT=w_sb[:, kc, oc * 128:(oc + 1) * 128], rhs=y_sb[:, kc, 0:w],
                                                  start=(kc == 0), stop=(kc == 7)),
                         reads=[tag + "w", tag + "y"], writes=["bank%d" % bi], pe_accum=True)
                j = 5 * 8 + oc
                k.op("dve", lambda e: e.scalar_tensor_tensor(out=x_sb[:, oc, 0:w], in0=pb[:, 0:w], scalar=g.mod[:, j, mc:mc + 1],
                                                            in1=x_sb[:, oc, 0:w], op0=ALU.mult, op1=ALU.add),
                     reads=["bank%d" % bi, "mod", tag + "x"], writes=[tag + "x"])
            k.dma("sp", xoutv[:, :, c0:c0 + w], x_sb[:, :, 0:w], reads=[tag + "x"])
        k.barrier()
        k.es = old


def build_dense(do_post, do_pre, final):
    nc = bass.Bass("TRN2", target_bir_lowering=False)
    I = lambda name, shape: nc.dram_tensor(name, list(shape), F32, kind="ExternalInput").ap()
    O = lambda name, shape: nc.dram_tensor(name, list(shape), F32, kind="ExternalOutput").ap()
    k = K(nc)
    g = setup_common(k)
    xT = I("xT", [D, T])
    x_cur = xT
    if do_post:
        modT_in = I("modT_in", [128, 144])
        yT = I("yT", [D, T])
        w_out = I("w_out", [D, D])
        f2g = I("ffn2_wg", [D, DFF]); f2u = I("ffn2_wu", [D, DFF]); f2d = I("ffn2_wd", [DFF, D])
        load_mod(k, g, modT_in)
        xmid = nc.dram_tensor("xmid", [D, T], F32).ap()
        outproj_phase(k, g, x_cur, yT, w_out, xmid)
        if final:
            fn = I("final_normT", [128, 8])
            xfin = O("x_out", [D, T])
            ffn_phase(k, g, xmid, xfin, f2g, f2u, f2d, 6, 7, 8, final_norm=fn, tag="g")
        else:
            xl = nc.dram_tensor("xl", [D, T], F32).ap()
            ffn_phase(k, g, xmid, xl, f2g, f2u, f2d, 6, 7, 8, tag="g")
            x_cur = xl
    if do_pre:
        cT = I("cT", [D, 2])
        ada_w = I("ada_w", [D, 9 * D])
        ada_bT = I("ada_bT", [128, 72])
        f1g = I("ffn1_wg", [D, DFF]); f1u = I("ffn1_wu", [D, DFF]); f1d = I("ffn1_wd", [DFF, D])
        w_inx = I("w_inx", [D, NPROJ])
        modT_out = O("modT_out", [128, 144])
        x_out = O("x_out", [D, T])
        projT = O("projT", [2048, T])
        if do_post:
            k.barrier()
        mod_phase(k, g, cT, ada_w, ada_bT, modT_out)
        ffn_phase(k, g, x_cur, x_out, f1g, f1u, f1d, 0, 1, 2, tag="f")
        proj_phase(k, g, x_out, w_inx, projT)
    k.finish()
    k.es.close()
    return nc


NK = 4352
MLA_SCALE = 96 ** -0.5
KCH = [(i * 512, 512) for i in range(8)] + [(4096, 256)]


def sgu_phase(k, g, sgT, ln_gT, ln_bT, sgwT, sgbT, identb, yT):
    with ExitStack() as es:
        old = k.es
        k.es = es
        z = k.sb([128, 4, 512], F32, name="sg_z")
        vb = k.sb([128, 2, 512], BF16, name="sg_vb")
        vsq = k.sb([128, 2, 512], BF16, name="sg_vsq")
        mean = k.sb([128, 512], F32, name="sg_mean")
        var = k.sb([128, 512], F32, name="sg_var")
        tmp = k.sb([128, 2, 512], F32, name="sg_tmp")
        vln = k.sb([128, 2, 512], BF16, name="sg_vln")
        vtok = k.sb([128, 256], BF16, name="sg_vtok")
        ysg = k.sb([128, 2, 512], F32, name="sg_y")
        t2 = k.sb([128, 128], F32, name="sg_t2")
        lg = k.sb([128, 2], F32, name="sg_lg")
        lb = k.sb([128, 2], F32, name="sg_lb")
        wT = k.sb([128, 4, 128], BF16, name="sg_wT")
        bT = k.sb([128, 2, 128], F32, name="sg_bT")
        k.dma("sp", lg[:], ln_gT, writes=["sg_lg"])
        k.dma("sp", lb[:], ln_bT, writes=["sg_lb"])
        k.dma("pool", wT[:], sgwT, writes=["sg_wT"])
        k.dma("sp", bT[:], sgbT, writes=["sg_bT"])
        sv = sgT.rearrange("(c p) t -> p c t", p=128)
        yv = yT[0:256, :].rearrange("(c p) t -> p c t", p=128)
        for (c0, w, mc) in TILES:
            k.dma("sp", z[:, :, 0:w], sv[:, :, c0:c0 + w], writes=["sg_z"])
            k.op("act", lambda e: e.activation(out=z[:, :, 0:w], in_=z[:, :, 0:w], func=AF.Gelu), reads=["sg_z"], writes=["sg_z"])
            k.op("dve", lambda e: e.tensor_copy(out=vb[:, :, 0:w], in_=z[:, 2:4, 0:w]), reads=["sg_z"], writes=["sg_vb"])
            k.op("pool", lambda e: e.tensor_tensor(out=vsq[:, :, 0:w], in0=z[:, 2:4, 0:w], in1=z[:, 2:4, 0:w], op=ALU.mult),
                 reads=["sg_z"], writes=["sg_vsq"])
            s1 = g.bank[0]
            s2 = g.bank[1]
            for kc in range(2):
                k.op("pe", lambda e: e.matmul(s1[:, 0:w], lhsT=g.ones[:], rhs=vb[:, kc, 0:w], start=(kc == 0), stop=(kc == 1)),
                     reads=["ones", "sg_vb"], writes=["bank0"], pe_accum=True)
            for kc in range(2):
                k.op("pe", lambda e: e.matmul(s2[:, 0:w], lhsT=g.ones[:], rhs=vsq[:, kc, 0:w], start=(kc == 0), stop=(kc == 1)),
                     reads=["ones", "sg_vsq"], writes=["bank1"], pe_accum=True)
            k.op("dve", lambda e: e.tensor_scalar(out=mean[:, 0:w], in0=s1[:, 0:w], scalar1=1.0 / 256, scalar2=None, op0=ALU.mult),
                 reads=["bank0"], writes=["sg_mean"])
            k.op("dve", lambda e: e.tensor_tensor(out=var[:, 0:w], in0=mean[:, 0:w], in1=mean[:, 0:w], op=ALU.mult),
                 reads=["sg_mean"], writes=["sg_var"])
            k.op("dve", lambda e: e.scalar_tensor_tensor(out=var[:, 0:w], in0=s2[:, 0:w], scalar=1.0 / 256, in1=var[:, 0:w],
                                                        op0=ALU.mult, op1=ALU.subtract),
                 reads=["bank1", "sg_var"], writes=["sg_var"])
            k.op("act", lambda e: e.activation(out=var[:, 0:w], in_=var[:, 0:w], func=AF.Sqrt, bias=g.epsb[:, 0:1], scale=1.0),
                 reads=["sg_var", "epsb"], writes=["sg_var"])
            k.op("dve", lambda e: e.reciprocal(out=var[:, 0:w], in_=var[:, 0:w]), reads=["sg_var"], writes=["sg_var"])
            for kc in range(2):
                k.op("dve", lambda e: e.tensor_tensor(out=tmp[:, kc, 0:w], in0=z[:, 2 + kc, 0:w], in1=mean[:, 0:w], op=ALU.subtract),
                     reads=["sg_z", "sg_mean"], writes=["sg_tmp"])
                k.op("pool", lambda e: e.tensor_tensor(out=tmp[:, kc, 0:w], in0=tmp[:, kc, 0:w], in1=var[:, 0:w], op=ALU.mult),
                     reads=["sg_tmp", "sg_var"], writes=["sg_tmp"])
                k.op("act", lambda e: e.activation(out=vln[:, kc, 0:w], in_=tmp[:, kc, 0:w], func=AF.Identity,
                                                   bias=lb[:, kc:kc + 1], scale=lg[:, kc:kc + 1]),
                     reads=["sg_tmp", "sg_lg", "sg_lb"], writes=["sg_vln"])
            for blk in range(w // 128):
                cs = slice(blk * 128, (blk + 1) * 128)
                ptb = g.bank[2][:].bitcast(BF16)
                for kc in range(2):
                    k.op("pe", lambda e: e.transpose(out=ptb[:, kc * 128:(kc + 1) * 128], in_=vln[:, kc, cs], identity=identb[:]),
                         reads=["sg_vln", "identb"], writes=["bank2"], pe_accum=True)
                k.op("act", lambda e: e.copy(out=vtok[:], in_=ptb[:, 0:256]), reads=["bank2"], writes=["sg_vtok"])
                for fc in range(2):
                    for gi in range(2):
                        gg = 2 * fc + gi
                        bi = 3 + gi
                        pb = g.bank[bi]
                        k.op("pe", lambda e: e.matmul(pb[:, 0:128], lhsT=vtok[:, fc * 128:(fc + 1) * 128], rhs=wT[:, gg, :],
                                                      start=True, stop=True),
                             reads=["sg_vtok", "sg_wT"], writes=["bank%d" % bi], pe_accum=True)
                        rs = slice(gi * 64, gi * 64 + 64)
                        k.op("dve", lambda e: e.tensor_tensor(out=t2[rs, :], in0=pb[rs, 0:128], in1=bT[rs, fc, :], op=ALU.add),
                             reads=["bank%d" % bi, "sg_bT"], writes=["sg_t2_%d" % gi])
                        k.op("pool", lambda e: e.tensor_tensor(out=ysg[rs, fc, cs], in0=t2[rs, :], in1=z[rs, fc, cs], op=ALU.mult),
                             reads=["sg_t2_%d" % gi, "sg_z"], writes=["sg_y"])
            k.dma("sp", yv[:, :, c0:c0 + w], ysg[:, :, 0:w], reads=["sg_y"])
        k.barrier()
        k.es = old


def attn_phase(k, g, A, yT, stage=9):
    with ExitStack() as es:
        old = k.es
        k.es = es
        identb = A["identb_sb"]
        identf = A["identf_sb"]
        KT = k.sb([96, 8, NK], BF16, name="KT")
        V = k.sb([128, 34, 512], BF16, name="V")
        w_uqx = k.sb([128, 3, 1536], BF16, name="w_uqx_sb")
        w_uk = k.sb([128, 2, 512], BF16, name="w_uk_sb")
        w_uv = k.sb([128, 2, 512], BF16, name="w_uv_sb")
        gq = k.sb([128, 3], F32, name="gq")
        gkv = k.sb([128, 2], F32, name="gkv")
        k.dma("pool", w_uqx[:], A["w_uqx"].rearrange("(kc p) n -> p kc n", p=128), writes=["w_uqx"])
        k.dma("pool", w_uk[:], A["w_uk"].rearrange("(kc p) n -> p kc n", p=128), writes=["w_uk"])
        k.dma("pool", w_uv[:], A["w_uv"].rearrange("(kc p) n -> p kc n", p=128), writes=["w_uv"])
        k.dma("sp", gq[:], A["gqT"], writes=["gq"])
        k.dma("sp", gkv[:], A["gkvT"], writes=["gkv"])
        with ExitStack() as es2:
            k.es = es2
            ckv = k.sb([128, 2, 512], F32, name="ckv")
            sq = k.sb([128, 2, 512], BF16, name="ksq")
            cn = k.sb([128, 2, 512], BF16, name="cn")
            rstd = k.sb([128, 512], F32, name="krstd")
            kr = k.sb([96, 512], F32, name="kr")
            krs = k.sb([96, 512], F32, name="krs")
            rc = k.sb([96, 512], F32, name="rc")
            rs_ = k.sb([96, 512], F32, name="rs")
            rot = k.sb([96, 512], BF16, name="rot")
            cv = A["ckvT"].rearrange("(c p) t -> p c t", p=128)
            for ti, (k0, kw) in enumerate(KCH):
                k.dma("sp", ckv[:, :, 0:kw], cv[:, :, k0:k0 + kw], writes=["ksrc"])
                k.dma("sp", kr[64:96, 0:kw], A["krT"][0:32, k0:k0 + kw], writes=["kr"])
                k.dma("sp", krs[64:96, 0:kw], A["krT"][32:64, k0:k0 + kw], writes=["krs"])
                k.dma("sp", rc[64:96, 0:kw], A["ropeK"][0:32, k0:k0 + kw], writes=["rc"])
                k.dma("sp", rs_[64:96, 0:kw], A["ropeK"][32:64, k0:k0 + kw], writes=["rs"])
                rms_rstd(k, g, ckv, 2, kw, 0, rstd, "k", sq)
                for kc in range(2):
                    k.op("dve", lambda e: e.scalar_tensor_tensor(out=cn[:, kc, 0:kw], in0=ckv[:, kc, 0:kw], scalar=gkv[:, kc:kc + 1],
                                                                in1=rstd[:, 0:kw], op0=ALU.mult, op1=ALU.mult),
                         reads=["ksrc", "gkv", "krstd"], writes=["cn"])
                k.op("pool", lambda e: e.tensor_tensor(out=kr[64:96, 0:kw], in0=kr[64:96, 0:kw], in1=rc[64:96, 0:kw], op=ALU.mult),
                     reads=["kr", "rc"], writes=["kr"])
                k.op("pool", lambda e: e.tensor_tensor(out=krs[64:96, 0:kw], in0=krs[64:96, 0:kw], in1=rs_[64:96, 0:kw], op=ALU.mult),
                     reads=["krs", "rs"], writes=["krs"])
                k.op("pool", lambda e: e.tensor_tensor(out=rot[64:96, 0:kw], in0=kr[64:96, 0:kw], in1=krs[64:96, 0:kw], op=ALU.add),
                     reads=["kr", "krs"], writes=["rot"])
                for h in range(8):
                    bi = 1 + (h % 4)
                    pb = g.bank[bi]
                    for kc in range(2):
                        k.op("pe", lambda e: e.matmul(pb[0:64, 0:kw], lhsT=w_uk[:, kc, h * 64:(h + 1) * 64], rhs=cn[:, kc, 0:kw],
                                                      start=(kc == 0), stop=(kc == 1)),
                             reads=["w_uk", "cn"], writes=["bank%d" % bi], pe_accum=True)
                    if h % 2 == 0:
                        k.op("act", lambda e: e.copy(out=KT[0:64, h, k0:k0 + kw], in_=pb[0:64, 0:kw]), reads=["bank%d" % bi], writes=["KT"])
                    else:
                        k.op("dve", lambda e: e.tensor_copy(out=KT[0:64, h, k0:k0 + kw], in_=pb[0:64, 0:kw]), reads=["bank%d" % bi], writes=["KT"])
                    k.op("pool", lambda e: e.tensor_copy(out=KT[64:96, h, k0:k0 + kw], in_=rot[64:96, 0:kw]), reads=["rot"], writes=["KT"])
                for blk in range(kw // 128):
                    bi = 5 + (blk % 2)
                    pb = g.bank[bi]
                    for kc in range(2):
                        k.op("pe", lambda e: e.matmul(pb[:, :], lhsT=cn[:, kc, blk * 128:(blk + 1) * 128], rhs=w_uv[:, kc, :],
                                                      start=(kc == 0), stop=(kc == 1)),
                             reads=["w_uv", "cn"], writes=["bank%d" % bi], pe_accum=True)
                    kb = k0 // 128 + blk
                    if blk % 2 == 0:
                        k.op("act", lambda e: e.copy(out=V[:, kb, :], in_=pb[:, :]), reads=["bank%d" % bi], writes=["V"])
                    else:
                        k.op("dve", lambda e: e.tensor_copy(out=V[:, kb, :], in_=pb[:, :]), reads=["bank%d" % bi], writes=["V"])
            k.barrier()
            k.es = es
        if stage <= 1:
            k.es = old
            return
        cq = k.sb([128, 3, 512], F32, name="cq")
        sq = k.sb([128, 3, 512], BF16, name="qsq")
        cqn = k.sb([128, 3, 512], BF16, name="cqn")
        rstd = k.sb([128, 512], F32, name="qrstd")
        rc = k.sb([96, 512], F32, name="qrc")
        rs_ = k.sb([96, 512], F32, name="qrs")
        t1 = k.sb([96, 512], F32, name="t1")
        t2 = k.sb([96, 512], F32, name="t2")
        QT = k.sb([96, 8, 512], BF16, name="QT")
        S = k.sb([128, NK], F32, name="S")
        P = k.sb([128, NK], BF16, name="P")
        PT = k.sb([128, 34, 128], BF16, name="PT")
        mx = k.sb([128, 4], F32, name="mx")
        Ytok = k.sb([128, 512], BF16, name="Ytok")
        YTs = k.sb([128, 4, 128], F32, name="YTs")
        cqv = A["cqT"].rearrange("(c p) t -> p c t", p=128)
        yv = yT[256:768, :].rearrange("(c p) t -> p c t", p=128)
        for (c0, w, mc) in TILES:
            k.dma("sp", cq[:, :, 0:w], cqv[:, :, c0:c0 + w], writes=["qsrc"])
            k.dma("sp", rc[64:96, 0:w], A["ropeQ"][0:32, c0:c0 + w], writes=["qrc"])
            k.dma("sp", rs_[64:96, 0:w], A["ropeQ"][32:64, c0:c0 + w], writes=["qrs"])
            rms_rstd(k, g, cq, 3, w, 0, rstd, "q", sq)
            for kc in range(3):
                k.op("dve", lambda e: e.scalar_tensor_tensor(out=cqn[:, kc, 0:w], in0=cq[:, kc, 0:w], scalar=gq[:, kc:kc + 1],
                                                            in1=rstd[:, 0:w], op0=ALU.mult, op1=ALU.mult),
                     reads=["qsrc", "gq", "qrstd"], writes=["cqn"])
            for h in range(8):
                ba = 1 + (h % 2)
                bb = 3 + (h % 2)
                pa = g.bank[ba]
                pbb = g.bank[bb]
                for kc in range(3):
                    k.op("pe", lambda e: e.matmul(pa[0:96, 0:w], lhsT=w_uqx[:, kc, h * 96:(h + 1) * 96], rhs=cqn[:, kc, 0:w],
                                                  start=(kc == 0), stop=(kc == 2)),
                         reads=["w_uqx", "cqn"], writes=["bank%d" % ba], pe_accum=True)
                for kc in range(3):
                    k.op("pe", lambda e: e.matmul(pbb[0:96, 0:w], lhsT=w_uqx[:, kc, 768 + h * 96:768 + (h + 1) * 96], rhs=cqn[:, kc, 0:w],
                                                  start=(kc == 0), stop=(kc == 2)),
                         reads=["w_uqx", "cqn"], writes=["bank%d" % bb], pe_accum=True)
                k.op("dve", lambda e: e.tensor_tensor(out=t1[64:96, 0:w], in0=pa[64:96, 0:w], in1=rc[64:96, 0:w], op=ALU.mult),
                     reads=["bank%d" % ba, "qrc"], writes=["t1"])
                k.op("dve", lambda e: e.tensor_tensor(out=t2[64:96, 0:w], in0=pbb[64:96, 0:w], in1=rs_[64:96, 0:w], op=ALU.mult),
                     reads=["bank%d" % bb, "qrs"], writes=["t2"])
                k.op("pool", lambda e: e.tensor_tensor(out=QT[64:96, h, 0:w], in0=t1[64:96, 0:w], in1=t2[64:96, 0:w], op=ALU.add),
                     reads=["t1", "t2"], writes=["QT"])
                k.op("act", lambda e: e.copy(out=QT[0:64, h, 0:w], in_=pa[0:64, 0:w]), reads=["bank%d" % ba], writes=["QT"])
            if stage <= 2:
                continue
            if mc == 0:
                kch = KCH
                nkk = NK
            else:
                kch = [(0, 256)]
                nkk = 256
            nkb = nkk // 128
            for qb in range(w // 128):
                qs = slice(qb * 128, (qb + 1) * 128)
                for h in range(8):
                    for ci, (k0, kw) in enumerate(kch):
                        bi = 1 + (ci % 2)
                        pb = g.bank[bi]
                        k.op("pe", lambda e: e.matmul(pb[:, 0:kw], lhsT=QT[:, h, qs], rhs=KT[:, h, k0:k0 + kw], start=True, stop=True),
                             reads=["QT", "KT"], writes=["bank%d" % bi], pe_accum=True)
                        k.op("act", lambda e: e.copy(out=S[:, k0:k0 + kw], in_=pb[:, 0:kw]), reads=["bank%d" % bi], writes=["S"])
                    k.op("dve", lambda e: e.tensor_reduce(out=mx[:, 0:1], in_=S[:, 0:nkk], axis=AX.X, op=ALU.max),
                         reads=["S"], writes=["mx0"])
                    k.op("dve", lambda e: e.tensor_scalar(out=mx[:, 1:2], in0=mx[:, 0:1], scalar1=-MLA_SCALE, scalar2=None, op0=ALU.mult),
                         reads=["mx0"], writes=["mx1"])
                    k.op("act", lambda e: e.activation(out=P[:, 0:nkk], in_=S[:, 0:nkk], func=AF.Exp, bias=mx[:, 1:2], scale=MLA_SCALE,
                                                       accum_out=mx[:, 2:3]),
                         reads=["S", "mx1"], writes=["P", "mx2"])
                    k.op("dve", lambda e: e.reciprocal(out=mx[:, 3:4], in_=mx[:, 2:3]), reads=["mx2"], writes=["mx3"])
                    if stage <= 3:
                        continue
                    for g4 in range((nkb + 3) // 4):
                        nb = min(4, nkb - g4 * 4)
                        bi = 3 + (g4 % 2)
                        ptb = g.bank[bi][:].bitcast(BF16)
                        for j in range(nb):
                            kb = g4 * 4 + j
                            k.op("pe", lambda e: e.transpose(out=ptb[:, j * 128:(j + 1) * 128], in_=P[:, kb * 128:(kb + 1) * 128], identity=identb[:]),
                                 reads=["P", "identb"], writes=["bank%d" % bi], pe_accum=True)
                        k.op("dve", lambda e: e.tensor_copy(out=PT[:, g4 * 4:g4 * 4 + nb, :].rearrange("p a b -> p (a b)"), in_=ptb[:, 0:nb * 128]),
                             reads=["bank%d" % bi], writes=["PT"])
                    if stage <= 4:
                        continue
                    bo = 5 + (h % 2)
                    po = g.bank[bo]
                    for kb in range(nkb):
                        k.op("pe", lambda e: e.matmul(po[:, 0:64], lhsT=PT[:, kb, :], rhs=V[:, kb, h * 64:(h + 1) * 64],
                                                      start=(kb == 0), stop=(kb == nkb - 1)),
                             reads=["PT", "V"], writes=["bank%d" % bo], pe_accum=True)
                    k.op("dve", lambda e: e.tensor_scalar(out=Ytok[:, h * 64:(h + 1) * 64], in0=po[:, 0:64], scalar1=mx[:, 3:4], scalar2=None,
                                                          op0=ALU.mult),
                         reads=["bank%d" % bo, "mx3"], writes=["Ytok"])
                if stage <= 5:
                    continue
                pt = g.bank[7][:].bitcast(BF16)
                for fc in range(4):
                    k.op("pe", lambda e: e.transpose(out=pt[:, fc * 128:(fc + 1) * 128], in_=Ytok[:, fc * 128:(fc + 1) * 128], identity=identb[:]),
                         reads=["Ytok", "identb"], writes=["bank7"], pe_accum=True)
                k.op("act", lambda e: e.copy(out=YTs[:].rearrange("p a b -> p (a b)"), in_=pt[:, 0:512]), reads=["bank7"], writes=["YTs"])
                k.dma("sp", yv[:, :, c0 + qb * 128:c0 + (qb + 1) * 128], YTs[:], reads=["YTs"])
        k.barrier()
        k.es = old


def build_attn(do_sgu=True, do_attn=True, stage=9):
    nc = bass.Bass("TRN2", target_bir_lowering=False)
    I = lambda name, shape: nc.dram_tensor(name, list(shape), F32, kind="ExternalInput").ap()
    O = lambda name, shape: nc.dram_tensor(name, list(shape), F32, kind="ExternalOutput").ap()
    k = K(nc)
    g = setup_common(k)
    A = {}
    for name, shape in [("ckvT", [256, NK]), ("krT", [64, NK]), ("ropeK", [64, NK]), ("cqT", [384, T]), ("ropeQ", [64, T]),
                        ("w_uqx", [384, 1536]), ("w_uk", [256, 512]), ("w_uv", [256, 512]), ("gqT", [128, 3]), ("gkvT", [128, 2])]:
        A[name] = I(name, shape)
    sgT = I("sgT", [512, T])
    ln_gT = I("ln_gT", [128, 2]); ln_bT = I("ln_bT", [128, 2])
    sgwT = I("sgwT", [128, 4, 128]); sgbT = I("sgbT", [128, 2, 128])
    ident = I("ident", [128, 128])
    yT = O("yT", [768, T])
    identb = k.sb([128, 128], BF16, name="identb")
    identf = k.sb([128, 128], F32, name="identf")
    k.dma("pool", identb[:], ident, writes=["identb"])
    k.dma("sp", identf[:], ident, writes=["identf"])
    A["identb_sb"] = identb
    A["identf_sb"] = identf
    if do_sgu:
        sgu_phase(k, g, sgT, ln_gT, ln_bT, sgwT, sgbT, identb, yT)
    if do_attn:
        attn_phase(k, g, A, yT, stage)
    k.finish()
    k.es.close()
    return nc


import math

TWO_PI = 2.0 * math.pi
OFFS = 32 * TWO_PI


def mm3(k, out, whi, wlo, xhi, xlo, reads, bank_key):
    k.op("pe", lambda e: e.matmul(out, lhsT=whi, rhs=xhi, start=True, stop=False), reads=reads, writes=[bank_key], pe_accum=True)
    k.op("pe", lambda e: e.matmul(out, lhsT=whi, rhs=xlo, start=False, stop=False), reads=reads, writes=[bank_key], pe_accum=True)
    k.op("pe", lambda e: e.matmul(out, lhsT=wlo, rhs=xhi, start=False, stop=True), reads=reads, writes=[bank_key], pe_accum=True)


def split_hl(k, src, hi, lo, keys_r, key_hi, key_lo):
    k.op("act", lambda e: e.copy(out=hi, in_=src), reads=keys_r, writes=[key_hi])
    k.op("dve", lambda e: e.tensor_tensor(out=lo, in0=src, in1=hi, op=ALU.subtract), reads=keys_r + [key_hi], writes=[key_lo])


def filter_phase(k, g, W, L, zT, zTr, winT, winTr, kf_d, kr_d, tag):
    nch = max(1, L // 512)
    cw = min(L, 512)
    with ExitStack() as es:
        old = k.es
        k.es = es
        KF = k.sb([64, 2 * L], BF16, name=tag + "KF")
        KR = k.sb([64, 2 * L], BF16, name=tag + "KR")
        zc = k.sb([33, 512], F32, name=tag + "zc")
        zhi = k.sb([33, 512], BF16, name=tag + "zhi")
        zlo = k.sb([33, 512], BF16, name=tag + "zlo")
        u = k.sb([64, 512], F32, name=tag + "u")
        h = k.sb([64, 512], F32, name=tag + "h")
        hhi = k.sb([64, 512], BF16, name=tag + "hhi")
        hlo = k.sb([64, 512], BF16, name=tag + "hlo")
        wn = k.sb([64, 512], F32, name=tag + "wn")
        cen = k.sb([64, 4], F32, name=tag + "cen")
        ni = k.sb([64, 512], mybir.dt.int32, name=tag + "ni")
        nf = k.sb([64, 512], F32, name=tag + "nf")
        cenb = k.sb([64, 2], BF16, name=tag + "cenb")
        k.op("dve", lambda e: e.memset(KF[:, 2 * L - 1:2 * L], 0.0), writes=[tag + "KF"])
        k.op("dve", lambda e: e.memset(KR[:, 2 * L - 1:2 * L], 0.0), writes=[tag + "KR"])
        for rev in range(2):
            zsrc = zTr if rev else zT
            wsrc = winTr if rev else winT
            for ci in range(nch):
                c0 = ci * 512
                k.dma("sp", zc[:, 0:cw], zsrc[:, c0:c0 + cw], writes=[tag + "zc"])
                k.dma("sp", wn[:, 0:cw], wsrc[:, c0:c0 + cw], writes=[tag + "wn"])
                split_hl(k, zc[:, 0:cw], zhi[:, 0:cw], zlo[:, 0:cw], [tag + "zc"], tag + "zhi", tag + "zlo")
                xin = (zhi[:, 0:cw], zlo[:, 0:cw], [tag + "zhi", tag + "zlo"])
                for layer in range(2):
                    whi, wlo = (W["w1hi"], W["w1lo"]) if layer == 0 else (W["w2hi"], W["w2lo"])
                    pb = g.bank[1 + layer]
                    mm3(k, pb[0:64, 0:cw], whi[:], wlo[:], xin[0], xin[1], xin[2] + ["hyW"], "bank%d" % (1 + layer))
                    k.op("dve", lambda e: e.tensor_scalar(out=u[:, 0:cw], in0=pb[0:64, 0:cw], scalar1=W["freq"][:, layer:layer + 1],
                                                          scalar2=W["fb"][:, layer:layer + 1], op0=ALU.mult, op1=ALU.add),
                         reads=["bank%d" % (1 + layer), "hyW"], writes=[tag + "u"])
                    k.op("dve", lambda e: e.tensor_scalar(out=ni[:, 0:cw], in0=u[:, 0:cw], scalar1=1.0 / TWO_PI, scalar2=None, op0=ALU.mult),
                         reads=[tag + "u"], writes=[tag + "ni"])
                    k.op("dve", lambda e: e.tensor_copy(out=nf[:, 0:cw], in_=ni[:, 0:cw]), reads=[tag + "ni"], writes=[tag + "nf"])
                    k.op("dve", lambda e: e.scalar_tensor_tensor(out=u[:, 0:cw], in0=nf[:, 0:cw], scalar=-TWO_PI, in1=u[:, 0:cw],
                                                                op0=ALU.mult, op1=ALU.add),
                         reads=[tag + "nf", tag + "u"], writes=[tag + "u"])
                    k.op("dve", lambda e: e.tensor_scalar(out=nf[:, 0:cw], in0=u[:, 0:cw], scalar1=math.pi, scalar2=TWO_PI, op0=ALU.is_gt, op1=ALU.mult),
                         reads=[tag + "u"], writes=[tag + "nf"])
                    k.op("dve", lambda e: e.tensor_tensor(out=u[:, 0:cw], in0=u[:, 0:cw], in1=nf[:, 0:cw], op=ALU.subtract),
                         reads=[tag + "nf", tag + "u"], writes=[tag + "u"])
                    k.op("act", lambda e: e.activation(out=h[:, 0:cw], in_=u[:, 0:cw], func=AF.Sin),
                         reads=[tag + "u"], writes=[tag + "h"])
                    split_hl(k, h[:, 0:cw], hhi[:, 0:cw], hlo[:, 0:cw], [tag + "h"], tag + "hhi", tag + "hlo")
                    xin = (hhi[:, 0:cw], hlo[:, 0:cw], [tag + "hhi", tag + "hlo"])
                for di in range(2):
                    whi, wlo = (W["w3fhi"], W["w3flo"]) if di == 0 else (W["w3bhi"], W["w3blo"])
                    pb = g.bank[3 + di]
                    mm3(k, pb[0:64, 0:cw], whi[:], wlo[:], xin[0], xin[1], xin[2] + ["hyW"], "bank%d" % (3 + di))
                    if rev == 0:
                        dst, dk = (KF, tag + "KF") if di == 0 else (KR, tag + "KR")
                        o0 = L - 1 + c0
                        if ci == 0:
                            k.op("dve", lambda e: e.tensor_tensor(out=cen[:, di:di + 1], in0=pb[0:64, 0:1], in1=wn[:, 0:1], op=ALU.mult),
                                 reads=["bank%d" % (3 + di), tag + "wn"], writes=[tag + "cen%d" % di])
                    else:
                        dst, dk = (KR, tag + "KR") if di == 0 else (KF, tag + "KF")
                        o0 = c0
                    k.op("dve", lambda e: e.tensor_tensor(out=dst[:, o0:o0 + cw], in0=pb[0:64, 0:cw], in1=wn[:, 0:cw], op=ALU.mult),
                         reads=["bank%d" % (3 + di), tag + "wn"], writes=[dk])
        k.op("dve", lambda e: e.tensor_tensor(out=cen[:, 2:3], in0=cen[:, 0:1], in1=cen[:, 1:2], op=ALU.add),
             reads=[tag + "cen0", tag + "cen1"], writes=[tag + "cen2"])
        k.op("dve", lambda e: e.tensor_tensor(out=cen[:, 3:4], in0=cen[:, 2:3], in1=W["bias"][:, 0:1], op=ALU.add),
             reads=[tag + "cen2", "hyW"], writes=[tag + "cen3"])
        k.op("dve", lambda e: e.tensor_copy(out=KF[:, L - 1:L], in_=cen[:, 3:4]), reads=[tag + "cen3"], writes=[tag + "KF"])
        k.op("dve", lambda e: e.tensor_copy(out=KR[:, L - 1:L], in_=cen[:, 3:4]), reads=[tag + "cen3"], writes=[tag + "KR"])
        k.dma("sp", kf_d, KF[:], reads=[tag + "KF"])
        k.dma("sp", kr_d, KR[:], reads=[tag + "KR"])
        k.barrier()
        k.es = old


def conv_phase(k, g, L, p3, cwt, cbt, kf_h, kr_h, identb, yout, tag):
    NB = L // 128
    SW = 2 * L - 128
    with ExitStack() as es:
        old = k.es
        k.es = es
        U = k.sb([128, L], F32, name=tag + "U")
        Z = k.sb([128, L], F32, name=tag + "Z")
        Zr = k.sb([128, L], F32, name=tag + "Zr")
        hi = k.sb([128, L], BF16, name=tag + "hi")
        lo = k.sb([128, L], BF16, name=tag + "lo")
        VT = k.sb([128, NB, 128], BF16, name=tag + "VT")
        V1T = k.sb([128, NB, 128], BF16, name=tag + "V1T")
        X1T = k.sb([128, NB, 128], F32, name=tag + "X1T")
        X2T = k.sb([128, NB, 128], F32, name=tag + "X2T")
        YO = k.sb([128, NB, 128], F32, name=tag + "YO")
        strip = [k.sb([128, SW], BF16, name=tag + "strip%d" % i) for i in range(2)]
        for s in range(3):
            k.dma("sp", U[:], p3[s], writes=[tag + "U"])
            k.op("dve", lambda e: e.tensor_scalar(out=Z[:], in0=U[:], scalar1=cwt[:, s, 1:2], scalar2=cbt[:, s:s + 1],
                                                  op0=ALU.mult, op1=ALU.add),
                 reads=[tag + "U", "cw"], writes=[tag + "Z"])
            k.op("dve", lambda e: e.scalar_tensor_tensor(out=Z[:, 1:L], in0=U[:, 0:L - 1], scalar=cwt[:, s, 0:1], in1=Z[:, 1:L],
                                                        op0=ALU.mult, op1=ALU.add),
                 reads=[tag + "U", "cw", tag + "Z"], writes=[tag + "Z"])
            k.op("dve", lambda e: e.scalar_tensor_tensor(out=Z[:, 0:L - 1], in0=U[:, 1:L], scalar=cwt[:, s, 2:3], in1=Z[:, 0:L - 1],
                                                        op0=ALU.mult, op1=ALU.add),
                 reads=[tag + "U", "cw", tag + "Z"], writes=[tag + "Z"])
            if s == 2:
                k.op("act", lambda e: e.copy(out=hi[:], in_=Z[:]), reads=[tag + "Z"], writes=[tag + "hi"])
                for b4 in range((NB + 7) // 8):
                    nb = min(8, NB - b4 * 8)
                    bi = 1 + (b4 % 2)
                    ptb = g.bank[bi][:].bitcast(BF16)
                    for j in range(nb):
                        blk = b4 * 8 + j
                        k.op("pe", lambda e: e.transpose(out=ptb[:, j * 128:(j + 1) * 128], in_=hi[:, blk * 128:(blk + 1) * 128], identity=identb[:]),
                             reads=[tag + "hi", "identb"], writes=["bank%d" % bi], pe_accum=True)
                    k.op("act", lambda e: e.copy(out=VT[:, b4 * 8:b4 * 8 + nb, :].rearrange("p a b -> p (a b)"), in_=ptb[:, 0:nb * 128]),
                         reads=["bank%d" % bi], writes=[tag + "VT"])
            else:
                if s == 0:
                    zt = Z[:]
                    zrev = bass.AP(zt.tensor, zt.offset + 127, [list(zt.ap[0]), [128, NB], [-1, 128]])
                    k.op("dve", lambda e: e.tensor_copy(out=Zr[:].rearrange("p (a b) -> p a b", b=128), in_=zrev),
                         reads=[tag + "Z"], writes=[tag + "Zr"])
                    src, sk, dstT, dk = Zr, tag + "Zr", X1T, tag + "X1T"
                else:
                    src, sk, dstT, dk = Z, tag + "Z", X2T, tag + "X2T"
                split_hl(k, src[:], hi[:], lo[:], [sk], tag + "hi", tag + "lo")
                for b4 in range((NB + 3) // 4):
                    nb = min(4, NB - b4 * 4)
                    bi = 1 + (b4 % 2)
                    pb = g.bank[bi]
                    for j in range(nb):
                        blk = b4 * 4 + j
                        k.op("pe", lambda e: e.matmul(pb[:, j * 128:(j + 1) * 128], lhsT=hi[:, blk * 128:(blk + 1) * 128], rhs=identb[:],
                                                      start=True, stop=False),
                             reads=[tag + "hi", "identb"], writes=["bank%d" % bi], pe_accum=True)
                        k.op("pe", lambda e: e.matmul(pb[:, j * 128:(j + 1) * 128], lhsT=lo[:, blk * 128:(blk + 1) * 128], rhs=identb[:],
                                                      start=False, stop=True),
                             reads=[tag + "lo", "identb"], writes=["bank%d" % bi], pe_accum=True)
                    k.op("act", lambda e: e.copy(out=dstT[:, b4 * 4:b4 * 4 + nb, :].rearrange("p a b -> p (a b)"), in_=pb[:, 0:nb * 128]),
                         reads=["bank%d" % bi], writes=[dk])
        for o in range(2):
            kh = kr_h if o == 0 else kf_h
            mov = VT if o == 0 else V1T
            mk = tag + "VT" if o == 0 else tag + "V1T"
            gate = X1T if o == 0 else X2T
            gk = tag + "X1T" if o == 0 else tag + "X2T"
            dstT = V1T if o == 0 else YO
            dk = tag + "V1T" if o == 0 else tag + "YO"
            for c in range(32):
                st = strip[c % 2]
                sk = tag + "strip%d" % (c % 2)
                row = o * 32 + c
                k.dma("sp", st[:], bass.AP(kh, row * 2 * L, [[1, 128], [1, SW]]), writes=[sk])
                c4 = c % 4
                bi = 3 + ((c // 4) % 2)
                pb = g.bank[bi]
                ncol = NB * 4
                dlist = [0] + [d for d in range(-(NB - 1), NB) if d != 0]
                for di, d in enumerate(dlist):
                    J0 = max(0, -d)
                    J1 = min(NB, NB - d)
                    I0 = J0 + d
                    nJ = J1 - J0
                    m0 = (L - 128 - 128 * d) if o == 0 else (128 * d + L - 128)
                    outap = pb[:, c4 * ncol + I0 * 4:c4 * ncol + (I0 + nJ) * 4].rearrange("p (i b) -> p i b", b=4)
                    k.op("pe", lambda e: e.matmul(outap, lhsT=st[:, m0:m0 + 128], rhs=mov[:, J0:J1, c * 4:(c + 1) * 4],
                                                  start=(di == 0), stop=(di == len(dlist) - 1)),
                         reads=[sk, mk], writes=["bank%d" % bi], pe_accum=True)
                if c4 == 3:
                    cg = c - 3
                    pv = pb[:, 0:4 * ncol].rearrange("p (c i b) -> p c i b", c=4, b=4)
                    gv = gate[:, :, cg * 4:cg * 4 + 16].rearrange("p i (c b) -> p c i b", b=4)
                    ov = dstT[:, :, cg * 4:cg * 4 + 16].rearrange("p i (c b) -> p c i b", b=4)
                    k.op("dve", lambda e: e.tensor_tensor(out=ov, in0=pv, in1=gv, op=ALU.mult),
                         reads=["bank%d" % bi, gk], writes=[dk])
        k.dma("sp", yout, YO[:], reads=[tag + "YO"])
        k.barrier()
        k.es = old


def build_hyena():
    nc = bass.Bass("TRN2", target_bir_lowering=False)
    I = lambda name, shape: nc.dram_tensor(name, list(shape), F32, kind="ExternalInput").ap()
    O = lambda name, shape: nc.dram_tensor(name, list(shape), F32, kind="ExternalOutput").ap()
    k = K(nc)
    g = setup_common(k)
    p3 = I("p3", [3, 128, 4096]); p3c = I("p3c", [3, 128, 256])
    cw = I("cw", [128, 3, 3]); cb = I("cb", [128, 3])
    zT = I("zT", [33, 4096]); zTr = I("zTr", [33, 4096]); winT = I("winT", [64, 4096]); winTr = I("winTr", [64, 4096])
    zcT = I("zcT", [33, 256]); zcTr = I("zcTr", [33, 256]); wincT = I("wincT", [64, 256]); wincTr = I("wincTr", [64, 256])
    w1 = I("hy_w1", [33, 64]); w2 = I("hy_w2", [64, 64]); w3f = I("w3f", [64, 64]); w3b = I("w3b", [64, 64])
    b12 = I("b12", [64, 2]); freq = I("freqT", [64, 2]); biasT = I("biasT", [64, 1])
    ident = I("ident", [128, 128])
    yh = O("yh", [128, 32, 128]); yhc = O("yhc", [128, 2, 128])
    kf_d = nc.dram_tensor("kf_d", [64, 8192], BF16); kr_d = nc.dram_tensor("kr_d", [64, 8192], BF16)
    kfc_d = nc.dram_tensor("kfc_d", [64, 512], BF16); krc_d = nc.dram_tensor("krc_d", [64, 512], BF16)
    identb = k.sb([128, 128], BF16, name="identb")
    k.dma("pool", identb[:], ident, writes=["identb"])
    cwt = k.sb([128, 3, 3], F32, name="cwt"); cbt = k.sb([128, 3], F32, name="cbt")
    k.dma("sp", cwt[:], cw, writes=["cw"])
    k.dma("sp", cbt[:], cb, writes=["cw"])
    W = {}
    raw = {}
    for nm, ap, shp in [("w1", w1, [33, 64]), ("w2", w2, [64, 64]), ("w3f", w3f, [64, 64]), ("w3b", w3b, [64, 64])]:
        t = k.sb(shp, F32, name="raw_" + nm)
        k.dma("sp", t[:], ap, writes=["raw_" + nm])
        W[nm + "hi"] = k.sb(shp, BF16, name=nm + "hi")
        W[nm + "lo"] = k.sb(shp, BF16, name=nm + "lo")
        split_hl(k, t[:], W[nm + "hi"][:], W[nm + "lo"][:], ["raw_" + nm], nm + "hi_k", nm + "lo_k")
    W["freq"] = k.sb([64, 2], F32, name="freq_sb")
    b12s = k.sb([64, 2], F32, name="b12_sb")
    W["fb"] = k.sb([64, 2], F32, name="fb_sb")
    W["bias"] = k.sb([64, 1], F32, name="bias_sb")
    W["twopi"] = k.sb([64, 512], F32, name="twopi")
    W["negpi"] = k.sb([64, 1], F32, name="negpi")
    k.dma("sp", W["freq"][:], freq, writes=["freq"])
    k.dma("sp", b12s[:], b12, writes=["b12"])
    k.dma("sp", W["bias"][:], biasT, writes=["biasr"])
    k.op("dve", lambda e: e.memset(W["twopi"][:], TWO_PI), writes=["twopi"])
    k.op("dve", lambda e: e.memset(W["negpi"][:], -math.pi), writes=["negpi"])
    k.op("dve", lambda e: e.tensor_tensor(out=W["fb"][:], in0=W["freq"][:], in1=b12s[:], op=ALU.mult), reads=["freq", "b12"], writes=["fb"])
    k.op("dve", lambda e: e.tensor_scalar(out=W["fb"][:], in0=W["fb"][:], scalar1=OFFS, scalar2=None, op0=ALU.add), reads=["fb"], writes=["fb"])
    k.barrier()
    filter_phase(k, g, W, 4096, zT, zTr, winT, winTr, kf_d.ap(), kr_d.ap(), "fl")
    filter_phase(k, g, W, 256, zcT, zcTr, wincT, wincTr, kfc_d.ap(), krc_d.ap(), "fc")
    conv_phase(k, g, 4096, p3, cwt, cbt, kf_d, kr_d, identb, yh, "cl")
    conv_phase(k, g, 256, p3c, cwt, cbt, kfc_d, krc_d, identb, yhc, "cc")
    k.finish()
    k.es.close()
    return nc


OFF_SG=768; OFF_Q=1280; OFF_KV=1664; OFF_KR=1920; N_IN=1952
SWAP = np.array([a*16 + (1-b)*8 + j for a in range(2) for b in range(2) for j in range(8)])
def core_tokens(x, ctx, core):
    b, half = core // 2, core % 2
    return np.ascontiguousarray(np.concatenate([x[b, half*2048:(half+1)*2048], ctx[b]], axis=0).T)
def dense_pre_inputs(inp, l, core):
    b = core // 2
    w_in = inp["w_in"][l]
    return {
        "cT": np.ascontiguousarray(np.stack([inp["c"][b], inp["c_ctx"]], axis=1)),
        "ada_w": inp["ada_w"][l], "ada_bT": np.ascontiguousarray(inp["ada_b"][l].reshape(72, 128).T),
        "ffn1_wg": inp["ffn1_wg"][l], "ffn1_wu": inp["ffn1_wu"][l], "ffn1_wd": inp["ffn1_wd"][l],
        "w_inx": np.ascontiguousarray(np.concatenate([w_in, w_in[:, OFF_KR + SWAP]], axis=1)),
    }

def rope_tables():
    rows = np.repeat(np.arange(64, dtype=np.float32), 64)
    cols = np.tile(np.arange(64, dtype=np.float32), 64)
    inv = (1.0 / (10000.0 ** (np.arange(8, dtype=np.float32) / 8))).astype(np.float32)
    RC = np.zeros((32, 4096), np.float32); RS = np.zeros((32, 4096), np.float32)
    for a in range(2):
        pos = rows if a == 0 else cols
        ang = (pos[None, :] * inv[:, None]).astype(np.float32)
        c = np.cos(ang).astype(np.float32); s = np.sin(ang).astype(np.float32)
        for bb in range(2):
            d0 = a * 16 + bb * 8
            RC[d0:d0 + 8] = c
            RS[d0:d0 + 8] = -s if bb == 0 else s
    return RC, RS

def attn_inputs(inp, l, core, P0, P1):
    half = core % 2
    Pown = P0 if half == 0 else P1
    full = np.concatenate([P0[:1984, :2048], P1[:1984, :2048]], axis=1)
    pc = P0[:1984, 2048:]
    RC, RS = rope_tables()
    one = np.ones((32, 256), np.float32); zero = np.zeros((32, 256), np.float32)
    ropeK = np.concatenate([np.concatenate([one, RC], 1), np.concatenate([zero, RS], 1)], 0)
    sl = slice(half * 2048, (half + 1) * 2048)
    ropeQ = np.concatenate([np.concatenate([RC[:, sl], one], 1), np.concatenate([RS[:, sl], zero], 1)], 0)
    w_uq = inp["mla_w_uq"][l]; w_ukv = inp["mla_w_ukv"][l]
    cols = list(range(768))
    for h in range(8):
        cols += list(h * 96 + np.arange(64)) + list(h * 96 + 64 + SWAP)
    w_uqx = w_uq[:, cols]
    kc = []
    for h in range(8):
        kc += list(h * 128 + np.arange(64))
    vc = []
    for h in range(8):
        vc += list(h * 128 + 64 + np.arange(64))
    sg_w = inp["sg_w"][l]; sg_b = inp["sg_b"][l]
    sgbT = np.zeros((128, 2, 128), np.float32)
    for fc in range(2):
        for gi in range(2):
            sgbT[gi * 64:(gi + 1) * 64, fc, :] = sg_b[2 * fc + gi][None, :]
    C = np.ascontiguousarray
    return {
        "ckvT": C(np.concatenate([pc[1664:1920], full[1664:1920]], 1)),
        "krT": C(np.concatenate([pc[1920:1984], full[1920:1984]], 1)),
        "ropeK": C(ropeK), "ropeQ": C(ropeQ),
        "cqT": C(Pown[1280:1664]), "sgT": C(Pown[768:1280]),
        "w_uqx": C(w_uqx), "w_uk": C(w_ukv[:, kc]), "w_uv": C(w_ukv[:, vc]),
        "gqT": C(inp["mla_q_norm"][l].reshape(3, 128).T), "gkvT": C(inp["mla_kv_norm"][l].reshape(2, 128).T),
        "ln_gT": C(inp["sg_ln_g"][l].reshape(2, 128).T), "ln_bT": C(inp["sg_ln_b"][l].reshape(2, 128).T),
        "sgwT": C(sg_w.transpose(2, 0, 1)), "sgbT": sgbT, "ident": np.eye(128, dtype=np.float32),
    }

def hy_consts(L):
    pos = np.arange(L, dtype=np.float32)
    t = (pos / max(L - 1, 1)).astype(np.float32)
    w = (2.0 * np.pi * pos / L).astype(np.float32)
    bands = np.linspace(1e-4, 15, 16, dtype=np.float32)
    ang = (w[:, None] * bands[None, :]).astype(np.float32)
    z = np.concatenate([t[:, None], np.cos(ang), -np.sin(ang)], axis=-1).astype(np.float32)
    min_decay = abs(np.log(1e-2) / 1.5); max_decay = abs(np.log(1e-2) / 0.3)
    deltas = np.linspace(min_decay, max_decay, 256, dtype=np.float32)
    win = (np.exp(-t[:, None] * deltas[None, :]) + 0.05).astype(np.float32)
    return z, win

def hyena_inputs(inp, l, core, PT_all):
    C = np.ascontiguousarray
    ch = np.arange(32 * core, 32 * core + 32)
    p3 = np.zeros((3, 128, 4096), np.float32); p3c = np.zeros((3, 128, 256), np.float32)
    for s in range(3):
        for b in range(4):
            rows = s * 256 + ch
            lat = np.concatenate([PT_all[2 * b][rows, :2048], PT_all[2 * b + 1][rows, :2048]], axis=1)
            p3[s, b::4, :] = lat
            p3c[s, b::4, :] = PT_all[2 * b][rows, 2048:]
    cwv = inp["hy_conv_w"][l]; cbv = inp["hy_conv_b"][l]
    cw = np.zeros((128, 3, 3), np.float32); cb = np.zeros((128, 3), np.float32)
    for s in range(3):
        for tap in range(3):
            cw[:, s, tap] = np.repeat(cwv[tap, s * 256 + ch], 4)
        cb[:, s] = np.repeat(cbv[s * 256 + ch], 4)
    z, win = hy_consts(4096); zc, winc = hy_consts(256)
    w3 = inp["hy_w3"][l]
    fcols = np.concatenate([o * 512 + ch for o in range(2)]); bcols = fcols + 256
    wsel = np.concatenate([ch, ch])
    return {
        "p3": p3, "p3c": p3c, "cw": cw, "cb": cb,
        "zT": C(z.T), "zTr": C(z[::-1].T), "winT": C(win[:, wsel].T), "winTr": C(win[::-1][:, wsel].T),
        "zcT": C(zc.T), "zcTr": C(zc[::-1].T), "wincT": C(winc[:, wsel].T), "wincTr": C(winc[::-1][:, wsel].T),
        "hy_w1": inp["hy_w1"][l], "hy_w2": inp["hy_w2"][l], "w3f": C(w3[:, fcols]), "w3b": C(w3[:, bcols]),
        "b12": C(np.stack([inp["hy_b1"][l], inp["hy_b2"][l]], 1)), "freqT": C(inp["hy_freq"][l].T),
        "biasT": C(inp["hy_bias"][l][:, ch].reshape(64, 1)), "ident": np.eye(128, dtype=np.float32),
    }


_PROGS = {}


def _prog(name, fn):
    if name not in _PROGS:
        _PROGS[name] = fn()
    return _PROGS[name]


def _run(nc, maps):
    res = run_bass_kernel_spmd(nc, maps, core_ids=list(range(8)))
    return res.results


def dense_post_inputs(inp, l, modT, yT_full):
    return {"modT_in": modT, "yT": yT_full, "w_out": inp["w_out"][l],
            "ffn2_wg": inp["ffn2_wg"][l], "ffn2_wu": inp["ffn2_wu"][l], "ffn2_wd": inp["ffn2_wd"][l]}


def assemble_y(yh_all, yhc_all, yT_c):
    YH = np.zeros((256, 4, 4096), np.float32)
    YHC = np.zeros((256, 4, 256), np.float32)
    for i in range(8):
        YH[32 * i:32 * i + 32] = yh_all[i].transpose(2, 1, 0).reshape(32, 4, 4096)
        YHC[32 * i:32 * i + 32] = yhc_all[i].transpose(2, 1, 0).reshape(32, 4, 256)
    outs = []
    for core in range(8):
        b, half = core // 2, core % 2
        hy = np.concatenate([YH[:, b, half * 2048:(half + 1) * 2048], YHC[:, b, :]], axis=1)
        outs.append(np.ascontiguousarray(np.concatenate([hy, yT_c[core]], axis=0)))
    return outs


def kernel(**inp):
    inp = {k_: np.asarray(v) for k_, v in inp.items()}
    x, ctx = inp["x"], inp["ctx"]
    xT = [core_tokens(x, ctx, c) for c in range(8)]
    modT = None
    yfull = None
    for l in range(3):
        do_post = l > 0
        do_pre = l < 2
        final = l == 2
        nc = _prog("dense_%d%d%d" % (do_post, do_pre, final), lambda: build_dense(do_post, do_pre, final))
        maps = []
        for c in range(8):
            m = {"xT": xT[c]}
            if do_post:
                m.update(dense_post_inputs(inp, l - 1, modT[c], yfull[c]))
                if final:
                    m["final_normT"] = np.ascontiguousarray(inp["final_norm"].reshape(8, 128).T)
            if do_pre:
                m.update(dense_pre_inputs(inp, l, c))
            maps.append(m)
        r = _run(nc, maps)
        xT = [r[c]["x_out"] for c in range(8)]
        if final:
            break
        modT = [r[c]["modT_out"] for c in range(8)]
        PT = [r[c]["projT"] for c in range(8)]
        nch = _prog("hyena", build_hyena)
        rh = _run(nch, [hyena_inputs(inp, l, c, PT) for c in range(8)])
        nca = _prog("attn", build_attn)
        ra = _run(nca, [attn_inputs(inp, l, c, PT[2 * (c // 2)], PT[2 * (c // 2) + 1]) for c in range(8)])
        yfull = assemble_y([rh[c]["yh"] for c in range(8)], [rh[c]["yhc"] for c in range(8)], [ra[c]["yT"] for c in range(8)])
    out = np.zeros((4, 4096, 1024), np.float32)
    for c in range(8):
        b, half = c // 2, c % 2
        out[b, half * 2048:(half + 1) * 2048, :] = xT[c][:, :2048].T
    return out
```
